# Optimizing a Trainium2 kernel written in Bass

```python
import jax, jax.numpy as jnp
from jax import lax
import numpy as np

D_MODEL = 1024
BATCH = 1
SEQ = 16384
DEPTH = 4
DEC_BATCH = 32
DEC_SEQ = 16
PAST_LEN = 1024

CHUNK = 64
D_PLE = 256
A_HEADS = 8
A_KV_HEADS = 2
A_HEAD_DIM = 64
A_GROUP = A_HEADS // A_KV_HEADS
A_WIDTH = A_HEADS * A_HEAD_DIM
A_KV_WIDTH = A_KV_HEADS * A_HEAD_DIM
WINDOW = 128
WINDOW_CHUNKS = WINDOW // CHUNK
ROT_DIM = A_HEAD_DIM // 4
ROPE_THETA = 500000.0
B_HEADS = 4
B_KEY_DIM = 64
B_VAL_DIM = 128
B_QK_WIDTH = B_HEADS * B_KEY_DIM
B_WIDTH = B_HEADS * B_VAL_DIM
GATE_RANK = 16
GATE_TAU = 16.0
GLA_BLOCK = 16
D_MIX = A_WIDTH + B_WIDTH
IN_SPLITS = (A_WIDTH, A_KV_WIDTH, A_KV_WIDTH, A_WIDTH, B_QK_WIDTH, B_QK_WIDTH, B_WIDTH, B_WIDTH, GATE_RANK)
IN_COLS = sum(IN_SPLITS)
NEG_INF = -1e30

kernel_name = "hymba_swa_sink_gla_streaming_step"


def _split_points():
    pts, acc = [], 0
    for n in IN_SPLITS[:-1]:
        acc += n
        pts.append(acc)
    return pts


def rms_norm(x, g, eps=1e-6):
    xf = x.astype(jnp.float32)
    y = xf * lax.rsqrt(jnp.mean(xf * xf, axis=-1, keepdims=True) + eps)
    return (y * g.astype(jnp.float32)).astype(x.dtype)


def partial_rope(x, pos):
    half = ROT_DIM // 2
    inv = jnp.power(jnp.float32(ROPE_THETA), -jnp.arange(half, dtype=jnp.float32) * (2.0 / ROT_DIM))
    ang = pos.astype(jnp.float32)[:, None] * inv[None, :]
    cos = jnp.cos(ang)[None, :, None, :]
    sin = jnp.sin(ang)[None, :, None, :]
    xr = x[..., :ROT_DIM].astype(jnp.float32)
    x1, x2 = xr[..., :half], xr[..., half:]
    rot = jnp.concatenate([x1 * cos - x2 * sin, x2 * cos + x1 * sin], axis=-1).astype(x.dtype)
    return jnp.concatenate([rot, x[..., ROT_DIM:]], axis=-1)


def sink_softmax_av(s, valid, sink, v, eq):
    if valid is not None:
        s = jnp.where(valid, s, NEG_INF)
    sk = sink.astype(jnp.float32)[:, :, None, None]
    m = jnp.maximum(jnp.max(s, axis=-1, keepdims=True), sk)
    pr = jnp.exp(s - m)
    den = jnp.sum(pr, axis=-1, keepdims=True) + jnp.exp(sk - m)
    return jnp.einsum(eq, pr / den, v.astype(jnp.float32))


def swa_prompt(q, k, v, sink):
    bn, L = q.shape[0], q.shape[1]
    nc = L // CHUNK
    band = (WINDOW_CHUNKS + 1) * CHUNK
    qc = q.reshape(bn, nc, CHUNK, A_KV_HEADS, A_GROUP, A_HEAD_DIM).astype(jnp.float32)
    pad = ((0, 0), (WINDOW_CHUNKS * CHUNK, 0), (0, 0), (0, 0))
    kp = jnp.pad(k, pad).reshape(bn, nc + WINDOW_CHUNKS, CHUNK, A_KV_HEADS, A_HEAD_DIM)
    vp = jnp.pad(v, pad).reshape(bn, nc + WINDOW_CHUNKS, CHUNK, A_KV_HEADS, A_HEAD_DIM)
    kb = jnp.concatenate([kp[:, j:j + nc] for j in range(WINDOW_CHUNKS + 1)], axis=2)
    vb = jnp.concatenate([vp[:, j:j + nc] for j in range(WINDOW_CHUNKS + 1)], axis=2)
    key_chunk = jnp.arange(nc)[:, None] - WINDOW_CHUNKS + (jnp.arange(band) // CHUNK)[None, :]
    valid = (key_chunk >= 0).reshape(1, nc, 1, 1, 1, band)
    s = jnp.einsum('bnqkgd,bnskd->bnkgqs', qc, kb.astype(jnp.float32)) * (A_HEAD_DIM ** -0.5)
    o = sink_softmax_av(s, valid, sink.reshape(A_KV_HEADS, A_GROUP), vb, 'bnkgqs,bnskd->bnqkgd')
    return o.reshape(bn, L, A_HEADS, A_HEAD_DIM).astype(q.dtype)


def swa_sample(q, k, v, ck, cv, sink):
    bn, T = q.shape[0], q.shape[1]
    kk = jnp.concatenate([ck.astype(k.dtype), k], axis=1).astype(jnp.float32)
    vv = jnp.concatenate([cv.astype(v.dtype), v], axis=1)
    qg = q.reshape(bn, T, A_KV_HEADS, A_GROUP, A_HEAD_DIM).astype(jnp.float32)
    s = jnp.einsum('bqkgd,bskd->bkgqs', qg, kk) * (A_HEAD_DIM ** -0.5)
    o = sink_softmax_av(s, None, sink.reshape(A_KV_HEADS, A_GROUP), vv, 'bkgqs,bskd->bqkgd')
    return o.reshape(bn, T, A_HEADS, A_HEAD_DIM).astype(q.dtype)


def gla_recurrent(q, k, v, logg, s0):
    bn, L, H, DK = q.shape
    DV = v.shape[-1]
    Lp = -(-L // GLA_BLOCK) * GLA_BLOCK
    padw = ((0, 0), (0, Lp - L), (0, 0), (0, 0))
    nb = Lp // GLA_BLOCK
    f = lambda a: jnp.pad(a.astype(jnp.float32), padw).reshape(bn, nb, GLA_BLOCK, H, a.shape[-1])
    qf, kf, vf, gf = f(q), f(k), f(v), f(logg)
    b = jnp.cumsum(gf, axis=2)
    b_last = b[:, :, -1:]
    qt = qf * jnp.exp(b)
    kt = kf * jnp.exp(-b)
    ke = kf * jnp.exp(b_last - b)
    causal = jnp.tril(jnp.ones((GLA_BLOCK, GLA_BLOCK), dtype=bool))
    att = jnp.einsum('bnthk,bnshk->bnhts', qt, kt)
    o_intra = jnp.einsum('bnhts,bnshv->bnthv', jnp.where(causal, att, 0.0), vf)

    def step(S, xs):
        qt_n, ke_n, v_n, dec_n = xs
        o = jnp.einsum('bthk,bhkv->bthv', qt_n, S)
        S = dec_n[..., None] * S + jnp.einsum('bthk,bthv->bhkv', ke_n, v_n)
        return S, o

    xs = (jnp.moveaxis(qt, 1, 0), jnp.moveaxis(ke, 1, 0), jnp.moveaxis(vf, 1, 0),
          jnp.moveaxis(jnp.exp(b_last[:, :, 0]), 1, 0))
    s_fin, o_inter = lax.scan(step, s0.astype(jnp.float32), xs)
    o = (o_intra + jnp.moveaxis(o_inter, 0, 1)).reshape(bn, Lp, H, DV)[:, :L]
    return o.astype(v.dtype), s_fin.astype(s0.dtype)


def trunk_layer(h, p_l, pos, ck, cv, s0, norm_g, w_in, q_norm_g, k_norm_g, sinks, w_gate_up, b_gate,
                gla_norm_g, w_out, pe_norm_g, w_pe, w_pg):
    bn, L = h.shape[0], h.shape[1]
    xn = rms_norm(h, norm_g)
    u = xn @ w_in
    qa, ka, va, ga, qb, kb, vb, gb, ab = jnp.split(u, _split_points(), axis=-1)
    qa = partial_rope(rms_norm(qa.reshape(bn, L, A_HEADS, A_HEAD_DIM), q_norm_g), pos)
    ka = partial_rope(rms_norm(ka.reshape(bn, L, A_KV_HEADS, A_HEAD_DIM), k_norm_g), pos)
    va = va.reshape(bn, L, A_KV_HEADS, A_HEAD_DIM)
    if ck is None:
        oa = swa_prompt(qa, ka, va, sinks)
        new_k, new_v = ka[:, L - WINDOW:], va[:, L - WINDOW:]
    else:
        oa = swa_sample(qa, ka, va, ck, cv, sinks)
        new_k, new_v = ka, va
    qb = qb.reshape(bn, L, B_HEADS, B_KEY_DIM) * (B_KEY_DIM ** -0.5)
    kb = kb.reshape(bn, L, B_HEADS, B_KEY_DIM)
    vb = vb.reshape(bn, L, B_HEADS, B_VAL_DIM)
    logg = jax.nn.log_sigmoid((ab @ w_gate_up + b_gate).astype(jnp.float32)) / GATE_TAU
    logg = logg.reshape(bn, L, B_HEADS, B_KEY_DIM)
    if s0 is None:
        s0 = jnp.zeros((bn, B_HEADS, B_KEY_DIM, B_VAL_DIM), h.dtype)
    ob, s_new = gla_recurrent(qb, kb, vb, logg, s0)
    ob = rms_norm(ob, gla_norm_g)
    mix = jnp.concatenate([oa.reshape(bn, L, A_WIDTH) * jax.nn.silu(ga),
                           ob.reshape(bn, L, B_WIDTH) * jax.nn.silu(gb)], axis=-1)
    h = h + mix @ w_out
    gate = jax.nn.sigmoid(rms_norm(h, pe_norm_g) @ w_pg)
    h = h + gate * (p_l @ w_pe)
    return h, new_k, new_v, s_new


def setup_inputs(seed: int = 0) -> dict:
    key = jax.random.key(seed)
    ks = jax.random.split(key, 20)
    f32 = jnp.float32
    nrm = lambda k, shape, s: jax.random.normal(k, shape, f32) * s
    swa_rows = min(WINDOW, PAST_LEN)
    return {
        "x_prompt": nrm(ks[0], (BATCH, SEQ, D_MODEL), 1.0),
        "x_sample": nrm(ks[1], (DEC_BATCH, DEC_SEQ, D_MODEL), 1.0),
        "cache_k": nrm(ks[2], (DEPTH, DEC_BATCH, swa_rows, A_KV_HEADS, A_HEAD_DIM), 1.0),
        "cache_v": nrm(ks[3], (DEPTH, DEC_BATCH, swa_rows, A_KV_HEADS, A_HEAD_DIM), 1.0),
        "state_gla": nrm(ks[4], (DEPTH, DEC_BATCH, B_HEADS, B_KEY_DIM, B_VAL_DIM), 0.5),
        "p_prompt": nrm(ks[5], (DEPTH, BATCH, SEQ, D_PLE), 1.0),
        "p_sample": nrm(ks[6], (DEPTH, DEC_BATCH, DEC_SEQ, D_PLE), 1.0),
        "norm_g": 1.0 + nrm(ks[7], (DEPTH, D_MODEL), 0.02),
        "w_in": nrm(ks[8], (DEPTH, D_MODEL, IN_COLS), D_MODEL ** -0.5),
        "q_norm_g": 1.0 + nrm(ks[9], (DEPTH, A_HEAD_DIM), 0.02),
        "k_norm_g": 1.0 + nrm(ks[10], (DEPTH, A_HEAD_DIM), 0.02),
        "sinks": nrm(ks[11], (DEPTH, A_HEADS), 0.5),
        "w_gate_up": nrm(ks[12], (DEPTH, GATE_RANK, B_QK_WIDTH), GATE_RANK ** -0.5),
        "b_gate": nrm(ks[13], (DEPTH, B_QK_WIDTH), 0.1),
        "gla_norm_g": 1.0 + nrm(ks[14], (DEPTH, B_VAL_DIM), 0.02),
        "w_out": nrm(ks[15], (DEPTH, D_MIX, D_MODEL), D_MIX ** -0.5),
        "pe_norm_g": 1.0 + nrm(ks[16], (DEPTH, D_MODEL), 0.02),
        "w_pe": nrm(ks[17], (DEPTH, D_PLE, D_MODEL), D_PLE ** -0.5),
        "w_pg": nrm(ks[18], (DEPTH, D_MODEL, D_MODEL), D_MODEL ** -0.5),
    }


def reference(x_prompt, x_sample, cache_k, cache_v, state_gla, p_prompt, p_sample, norm_g, w_in,
              q_norm_g, k_norm_g, sinks, w_gate_up, b_gate, gla_norm_g, w_out, pe_norm_g, w_pe, w_pg):
    pos_prompt = jnp.arange(x_prompt.shape[1], dtype=jnp.int32)
    pos_sample = PAST_LEN + jnp.arange(x_sample.shape[1], dtype=jnp.int32)
    hp, hs = x_prompt, x_sample
    kp_l, vp_l, sp_l, ks_l, vs_l, ss_l = [], [], [], [], [], []
    for i in range(DEPTH):
        lw = (norm_g[i], w_in[i], q_norm_g[i], k_norm_g[i], sinks[i], w_gate_up[i], b_gate[i],
              gla_norm_g[i], w_out[i], pe_norm_g[i], w_pe[i], w_pg[i])
        hp, kn, vn, sn = trunk_layer(hp, p_prompt[i], pos_prompt, None, None, None, *lw)
        kp_l.append(kn); vp_l.append(vn); sp_l.append(sn)
        hs, kn, vn, sn = trunk_layer(hs, p_sample[i], pos_sample, cache_k[i], cache_v[i], state_gla[i], *lw)
        ks_l.append(kn); vs_l.append(vn); ss_l.append(sn)
    return (hp, hs, jnp.stack(kp_l), jnp.stack(vp_l), jnp.stack(sp_l),
            jnp.stack(ks_l), jnp.stack(vs_l), jnp.stack(ss_l))
```

```python
import numpy as np
from contextlib import ExitStack
import concourse.bass as bass
import concourse.mybir as mybir
from concourse.bass_utils import run_bass_kernel_spmd

F32 = mybir.dt.float32
BF16 = mybir.dt.bfloat16
ALU = mybir.AluOpType
AF = mybir.ActivationFunctionType
AX = mybir.AxisListType

NCORES = 8
DM = 1024
DEPTH = 4
SEQ = 16384
TPC = SEQ // NCORES
NPT = TPC // 128
NT = NPT + 1
TOK = NT * 128
INC = 2832
PAST = 1024
EPS = 1e-6
PAYW = 516

ENGS = ("pe", "act", "dve", "pool", "sp")
SELF_SYNC = {"pe": False, "act": True, "dve": True, "pool": True, "sp": False}


class Buf:
    __slots__ = ("name", "writer", "readers")

    def __init__(self, name):
        self.name = name
        self.writer = None
        self.readers = []


class Op:
    __slots__ = ("eng", "fn", "deps", "is_dma", "semkey", "marked", "count", "idx", "cc")

    def __init__(self, eng, fn, is_dma=False, semkey=None, cc=False):
        self.eng = eng
        self.fn = fn
        self.deps = set()
        self.is_dma = is_dma
        self.semkey = semkey
        self.marked = False
        self.count = 0
        self.idx = -1
        self.cc = cc


class Prog:
    def __init__(self):
        self.ops = []

    def op(self, eng, fn, reads=(), writes=(), dma=False, semkey=None, cc=False, writes_nt=()):
        import os as _os
        if len(self.ops) >= int(_os.environ.get("KOPS", "100000000")):
            return None
        o = Op(eng, fn, is_dma=dma, semkey=semkey, cc=cc)
        o.idx = len(self.ops)
        for b in reads:
            if b.writer is not None:
                o.deps.add(b.writer)
        for b in list(writes) + list(writes_nt):
            if b.writer is not None:
                o.deps.add(b.writer)
            for r in b.readers:
                o.deps.add(r)
        for b in reads:
            b.readers.append(o.idx)
        for b in writes:
            b.writer = o.idx
            b.readers = []
        o.deps.discard(o.idx)
        self.ops.append(o)
        return o

    def emit(self, nc):
        ops = self.ops
        for o in ops:
            for d in o.deps:
                p = ops[d]
                if p.is_dma or p.cc:
                    continue
                if p.eng == o.eng and not o.is_dma and not SELF_SYNC[p.eng]:
                    continue
                p.marked = True
        eng_cnt = {e: 0 for e in ENGS}
        dma_cnt = {}
        for o in ops:
            if o.is_dma or o.cc:
                k = o.semkey
                dma_cnt[k] = dma_cnt.get(k, 0) + (16 if o.is_dma else 1)
                o.count = dma_cnt[k]
            elif o.marked:
                eng_cnt[o.eng] += 1
                o.count = eng_cnt[o.eng]
        final_dma = dict(dma_cnt)
        with ExitStack() as es:
            sems = {}
            for e in ENGS:
                sems[e] = es.enter_context(nc.semaphore("s_" + e))
            for k in dma_cnt:
                sems[("dma", k)] = es.enter_context(nc.semaphore("d_" + str(k)))
            block = es.enter_context(nc.Block())
            per_eng = {e: [o for o in ops if o.eng == e] for e in ENGS}

            def run_engine(ename, eng):
                known = {}
                for o in per_eng[ename]:
                    need = {}
                    for d in o.deps:
                        p = ops[d]
                        if p.is_dma or p.cc:
                            key = ("dma", p.semkey)
                        else:
                            if p.eng == ename and not o.is_dma and not SELF_SYNC[ename]:
                                continue
                            key = p.eng
                        if p.count > need.get(key, 0):
                            need[key] = p.count
                    for key, v in need.items():
                        if known.get(key, 0) >= v:
                            continue
                        eng.wait_ge(sems[key], v)
                        known[key] = v
                    ins = o.fn(eng)
                    if o.is_dma:
                        ins.then_inc(sems[("dma", o.semkey)], 16)
                    elif o.cc:
                        ins.then_inc(sems[("dma", o.semkey)], 1)
                    elif o.marked:
                        ins.then_inc(sems[ename], 1)
                if ename == "sp":
                    for k, v in final_dma.items():
                        eng.wait_ge(sems[("dma", k)], v)

            @block.tensor
            def _(e):
                run_engine("pe", e)

            @block.scalar
            def _(e):
                run_engine("act", e)

            @block.vector
            def _(e):
                run_engine("dve", e)

            @block.gpsimd
            def _(e):
                run_engine("pool", e)

            @block.sync
            def _(e):
                run_engine("sp", e)


def bcast(ap, dims):
    a = list(ap.ap)
    return bass.AP(ap.tensor, ap.offset, [list(a[0])] + [list(d) for d in dims])


def build(depth=DEPTH):
    nc = bass.Bass("TRN2", target_bir_lowering=False)
    P = Prog()

    def din(name, shape):
        return nc.dram_tensor(name, shape, F32, kind="ExternalInput").ap()

    def dout(name, shape):
        return nc.dram_tensor(name, shape, F32, kind="ExternalOutput").ap()

    xT_d = din("xT", [DM, TOK])
    pT_d = din("pT", [depth, 256, TOK])
    ckT_d = din("ckT", [depth, 4, 128, 128])
    cv_d = din("cv", [depth, 4, 128, 128])
    s0_d = din("s0", [depth, 4, 4, 64, 128])
    cos_d = din("cosT", [128, NT * 8])
    sin_d = din("sinT", [128, NT * 8])
    cst_d = din("cst", [128, 10 * 128])
    selmsk_d = din("selmsk", [128, 20])
    ng_d = din("ng", [128, depth * 8])
    peg_d = din("peg", [128, depth * 8])
    qkg_d = din("qkg", [128, depth * 640])
    gng_d = din("gng", [128, depth * 512])
    snk_d = din("snk", [128, depth * 8])
    wg_d = din("wg", [32, depth * 256])
    w_in_d = din("w_in", [depth, DM, INC])
    w_out_d = din("w_out", [depth, DM, DM])
    w_pg_d = din("w_pg", [depth, DM, DM])
    w_pe_d = din("w_pe", [depth, 256, DM])

    yT_o = dout("yT", [DM, TOK])
    nk_o = dout("nk", [depth, 2, 128, 128])
    nv_o = dout("nv", [depth, 2, 128, 128])
    sp_o = dout("st_p", [depth, 128, 256])
    ss_o = dout("st_s", [depth, 128, 1024])

    spill_d = nc.dram_tensor("spill", [NT, 128, 1792], BF16).ap()
    ag_in_d = [nc.dram_tensor("ag_in%d" % l, [128, PAYW], F32).ap() for l in range(depth)]
    ag_out_d = [nc.dram_tensor("ag_out%d" % l, [NCORES * 128, PAYW], F32).ap() for l in range(depth)]

    with ExitStack() as es:
        def sb(name, shape, dt=F32):
            return es.enter_context(nc.sbuf_tensor(name, shape, dt))

        hT = sb("hT", [128, 8, TOK])
        WX = sb("WX", [128, 8, 1280], BF16)
        WY = sb("WY", [128, 8, 1552], BF16)
        cst = sb("cst_sb", [128, 10 * 128], BF16)
        ident, ones_bf = cst[:, 0:128], cst[:, 128:256]
        U_p, U_s, mG_p, mG_s = cst[:, 256:384], cst[:, 384:512], cst[:, 512:640], cst[:, 640:768]
        cos_sb = sb("cos_sb", [128, NT * 8])
        sin_sb = sb("sin_sb", [128, NT * 8])
        selmsk = sb("selmsk_sb", [128, 20])
        qmask = sb("qmask", [128, 4, 256], BF16)
        ng_sb = sb("ng_sb", [128, depth * 8])
        peg_sb = sb("peg_sb", [128, depth * 8])
        qkg_sb = sb("qkg_sb", [128, 640])
        gng_sb = sb("gng_sb", [128, 512])
        esink = sb("esink", [128, depth * 8])
        wg_sb = sb("wg_sb", [32, depth * 256], BF16)
        abT_aug = sb("abT_aug", [32, 128], BF16)

        SCR = sb("SCR", [128, 2304])
        eg, eg2, scr1, scr2 = SCR[:, 0:512], SCR[:, 512:1024], SCR[:, 1024:1664], SCR[:, 1664:2304]
        gath = SCR[:, 0:4 * PAYW].rearrange("p (r n) -> p r n", r=4)

        GN = 256
        sqg = [sb("sqg%d" % i, [128, 8, GN], BF16) for i in range(2)]
        rstdg = [sb("rstdg%d" % i, [128, GN]) for i in range(2)]
        xng = [sb("xng%d" % i, [128, 8, GN], BF16) for i in range(2)]
        ss = sb("ss", [128, 16])
        rq = sb("rq", [128, 16])
        rt = sb("rt", [128, 4, 10, 8])
        qtm = sb("qtm", [128, 5, 128], BF16)
        qT = sb("qT", [128, 4, 128], BF16)
        qT0 = sb("qT0", [128, 4, 128], BF16)
        kT0 = sb("kT0", [128, 128], BF16)
        kTr = [sb("kTr%d" % i, [128, 128], BF16) for i in range(2)]
        kTh = sb("kTh", [128, 128], BF16)
        vaug0 = sb("vaug0", [128, 2, 65], BF16)
        vaugr = [sb("vaugr%d" % i, [128, 2, 65], BF16) for i in range(2)]
        vaugh = sb("vaugh", [128, 2, 65], BF16)
        vf32 = sb("vf32", [128, 128])
        sga = sb("sga", [128, 512], BF16)
        sga0 = sb("sga0", [128, 512], BF16)
        lg = sb("lg", [128, 256])
        lhi = sb("lhi", [128, 256], BF16)
        llo = sb("llo", [128, 256], BF16)
        ebT = sb("ebT", [128, 256])
        enbT = sb("enbT", [128, 256])
        enb = sb("enb", [128, 256])
        ktT = sb("ktT", [128, 256], BF16)
        kt = sb("kt", [128, 256], BF16)
        vbb = sb("vbb", [128, 512], BF16)
        attb = sb("attb", [128, 4, 128], BF16)
        S = sb("S", [128, 256])
        Sbf = sb("Sbf", [128, 2, 128], BF16)
        Dall = sb("Dall", [128, NT, 2])
        Dtot = sb("Dtot", [128, 2])
        Ds = sb("Ds", [128, 4, 2])
        S0f = sb("S0f", [128, 4, 2, 128])
        S0bf = sb("S0bf", [128, 4, 2, 128], BF16)
        ckT_bf = sb("ckT_bf", [128, 4, 128], BF16)
        cvaug = sb("cvaug", [128, 4, 2, 65], BF16)
        pTs_all = sb("pTs_all", [128, 4, 512], BF16)
        pTs = [pTs_all[:, i, :] for i in range(4)]
        vm = pTs_all
        pTn = sb("pTn", [128, 256], BF16)
        den = sb("den", [128, 8])
        spo = [sb("spo%d" % i, [128, 1792], BF16) for i in range(2)]
        spoA = [spo[i][:, 0:512] for i in range(2)]
        spoB = [spo[i][:, 512:1792] for i in range(2)]
        spi = spo
        pay = sb("pay", [128, PAYW])
        hk = sb("hk", [128, 258])
        fac = sb("fac", [128, NCORES, 2])
        Sc = sb("Sc", [128, 256])
        Scbf = sb("Scbf", [128, 2, 128], BF16)
        Send = sb("Send", [128, 256])
        oss = sb("oss", [128, 8])
        ro = sb("ro", [128, 8])
        mixb = sb("mixb", [128, 512], BF16)
        pTf = sb("pTf", [128, 2, GN])
        pTb = [sb("pTb%d" % i, [128, 2, GN], BF16) for i in range(2)]

        PB = [es.enter_context(nc.psum_tensor("pb%d" % i, [128, 512], F32)) for i in range(7)]
        PT7 = es.enter_context(nc.psum_tensor("pb7", [128, 8, 128], BF16))

        bufs = {}

        def B(name):
            if name not in bufs:
                bufs[name] = Buf(name)
            return bufs[name]

        def pbq(b, lo=0, hi=512):
            return [B("pb%d" % b)]

        hb = [B("hT%d" % t) for t in range(NT)]
        SCRB = [B("eg"), B("eg2"), B("scr1"), B("scr2")]
        WXb = [B("WX%d" % kc) for kc in range(8)]
        WYb = [B("WY%d" % kc) for kc in range(8)]

        def mm(out, lhsT, rhs, start, stop, r, w, tp=None):
            if tp is None:
                P.op("pe", lambda e: e.matmul(out, lhsT=lhsT, rhs=rhs, start=start, stop=stop), reads=r, writes=w)
            else:
                P.op("pe", lambda e: e.matmul(out, lhsT=lhsT, rhs=rhs, start=start, stop=stop, tile_position=tp), reads=r, writes=w)

        def tr(out, in_, r, w):
            P.op("pe", lambda e: e.transpose(out=out, in_=in_, identity=ident), reads=r + [B("cst")], writes=w)

        def act(out, in_, func, r, w, scale=1.0, bias=0.0):
            P.op("act", lambda e: e.activation(out=out, in_=in_, func=func, bias=bias, scale=scale), reads=r, writes=w)

        def tt(out, in0, in1, op, r, w, eng="dve"):
            P.op(eng, lambda e: e.tensor_tensor(out=out, in0=in0, in1=in1, op=op), reads=r, writes=w)

        def ts(out, in0, s1, op0, r, w, eng="dve"):
            P.op(eng, lambda e: e.tensor_scalar(out=out, in0=in0, scalar1=s1, scalar2=None, op0=op0), reads=r, writes=w)

        def stt(out, in0, scalar, in1, op0, op1, r, w, eng="dve"):
            P.op(eng, lambda e: e.scalar_tensor_tensor(out=out, in0=in0, scalar=scalar, in1=in1, op0=op0, op1=op1), reads=r, writes=w)

        def cp(out, in_, r, w, eng="dve"):
            if eng == "act":
                P.op("act", lambda e: e.activation(out=out, in_=in_, func=AF.Copy), reads=r, writes=w)
            else:
                P.op(eng, lambda e: e.tensor_copy(out=out, in_=in_), reads=r, writes=w)

        def red(out, in_, r, w):
            P.op("dve", lambda e: e.tensor_reduce(out=out, in_=in_, axis=AX.X, op=ALU.add), reads=r, writes=w)

        def rcp(out, in_, r, w):
            P.op("dve", lambda e: e.reciprocal(out=out, in_=in_), reads=r, writes=w)

        def mset(ap, val, w, eng="dve"):
            P.op(eng, lambda e: e.memset(ap, val), writes=w)

        def dma(eng, out, in_, r, w, key, wnt=()):
            P.op(eng, lambda e: e.dma_start(out=out, in_=in_), reads=r, writes=w, dma=True, semkey=key, writes_nt=wnt)

        for kc in range(8):
            dma("sp", hT[:, kc, :], xT_d[kc * 128:(kc + 1) * 128, :], [], hb if kc == 7 else [], "x", wnt=hb)
        dma("pool", cst[:], cst_d, [], [B("cst")], "cst")
        dma("pool", wg_sb[:], wg_d, [], [B("wg")], "wg")
        dma("sp", cos_sb[:], cos_d, [], [B("cos")], "cos")
        dma("sp", sin_sb[:], sin_d, [], [B("sin")], "sin")
        dma("sp", selmsk[:], selmsk_d, [], [B("selmsk")], "selmsk")
        dma("sp", ng_sb[:], ng_d, [], [B("ng")], "ng")
        dma("sp", peg_sb[:], peg_d, [], [B("peg")], "peg")
        dma("sp", esink[:], snk_d, [], [B("esink")], "snk")
        act(esink[:], esink[:], AF.Exp, [B("esink")], [B("esink")])
        mset(abT_aug[:], 1.0, [B("abT_aug")])
        for t_, n_ in ((vaug0, "vaug0"), (vaugr[0], "vaugr0"), (vaugr[1], "vaugr1"), (cvaug, "cvaug")):
            mset(t_[:], 1.0, [B(n_)])

        def load_wx_in(l):
            for kc in range(8):
                dma("pool", WX[:, kc, :], w_in_d[l, kc * 128:(kc + 1) * 128, 0:1280], [], [WXb[kc]], "WX%d" % kc)

        def load_wy_in(l):
            for kc in range(8):
                dma("pool", WY[:, kc, :], w_in_d[l, kc * 128:(kc + 1) * 128, 1280:2832], [], [WYb[kc]], "WY%d" % kc)

        def load_wx_out(l):
            for kc in range(8):
                dma("pool", WX[:, kc, 0:1024], w_out_d[l, kc * 128:(kc + 1) * 128, :], [], [WXb[kc]], "WX%d" % kc)

        def load_wy_pg(l):
            for kc in range(8):
                if kc < 4:
                    dma("pool", WY[:, kc, 1024:1536], w_pe_d[l, (kc // 2) * 128:(kc // 2 + 1) * 128, (kc % 2) * 512:(kc % 2 + 1) * 512], [], [],
                        "WY%d" % kc, wnt=[WYb[kc]])
                dma("pool", WY[:, kc, 0:1024], w_pg_d[l, kc * 128:(kc + 1) * 128, :], [], [WYb[kc]], "WY%d" % kc)

        def load_layer_small(l):
            dma("sp", qkg_sb[:], qkg_d[:, l * 640:(l + 1) * 640], [], [B("qkg")], "qkg")
            dma("sp", gng_sb[:], gng_d[:, l * 512:(l + 1) * 512], [], [B("gng")], "gng")
            dma("pool", ckT_bf[:], ckT_d[l].rearrange("b p k -> p b k"), [], [B("ckT")], "ckT")
            for b in range(4):
                dma("pool", cvaug[:, b, :, 0:64], cv_d[l, b].rearrange("k (g d) -> k g d", g=2), [], [B("cvaug")] if b == 3 else [], "cv",
                    wnt=[B("cvaug")])
            for b in range(4):
                s0v = s0_d[l, b].rearrange("(hc hp) k v -> hp k hc v", hp=2)
                for hp in range(2):
                    lastd = (b == 3 and hp == 1)
                    dma("sp", S0f[hp * 64:(hp + 1) * 64, b, :, :], s0v[hp], [], [B("S0f")] if lastd else [], "s0", wnt=[B("S0f")])
            cp(S0bf[:], S0f[:], [B("S0f")], [B("S0bf")])

        GROUPS = [[2 * i, 2 * i + 1] for i in range(NPT // 2)] + [[NPT]]

        def norm_group(gi, gsb, l, rbank=6):
            tiles = GROUPS[gi]
            k = gi % 2
            c0, N = tiles[0] * 128, 128 * len(tiles)
            hbs = [hb[t] for t in tiles]
            sqb, rsb, xnb = B("sqg%d" % k), B("rstdg%d" % k), B("xng%d" % k)
            act(sqg[k][:, :, 0:N], hT[:, :, c0:c0 + N], AF.Square, hbs, [sqb])
            for kc in range(8):
                mm(PB[rbank][:, 0:N], ones_bf, sqg[k][:, kc, 0:N], kc == 0, kc == 7, [sqb, B("cst")], pbq(rbank))
            act(rstdg[k][:, 0:N], PB[rbank][:, 0:N], AF.Ln, pbq(rbank), [rsb], scale=1.0 / DM, bias=EPS)
            act(rstdg[k][:, 0:N], rstdg[k][:, 0:N], AF.Exp, [rsb], [rsb], scale=-0.5)
            for kc in range(8):
                stt(xng[k][:, kc, 0:N], hT[:, kc, c0:c0 + N], gsb[:, l * 8 + kc:l * 8 + kc + 1], rstdg[k][:, 0:N], ALU.mult, ALU.mult,
                    hbs + [rsb, B("ng"), B("peg")], [xnb])

        def sigmoid_act(dst, dstb, src, srcb):
            act(dst, src, AF.Exp, srcb, [dstb], scale=-1.0)
            act(dst, dst, AF.Ln, [dstb], [dstb], bias=1.0)
            act(dst, dst, AF.Exp, [dstb], [dstb], scale=-1.0)

        oan = scr2[:, 0:512]
        OAB = (6, 1)

        def swa_finish(l, dst, dstbuf, sga_t, sga_buf):
            for g in range(2):
                v = PB[OAB[g]][:].rearrange("p (j c) -> p j c", j=4)
                tt(den[:, g * 4:(g + 1) * 4], v[:, :, 64], esink[:, l * 8 + g * 4:l * 8 + g * 4 + 4], ALU.add,
                   pbq(OAB[g]) + [B("esink")], [B("den")])
            rcp(den[:], den[:], [B("den")], [B("den")])
            for g in range(2):
                for j in range(4):
                    h = g * 4 + j
                    P.op("act", lambda e, h=h, g=g, j=j: e.activation(out=oan[:, h * 64:(h + 1) * 64], in_=PB[OAB[g]][:, j * 128:j * 128 + 64],
                                                                      func=AF.Copy, scale=den[:, h:h + 1]),
                         reads=pbq(OAB[g], j * 128, j * 128 + 128) + [B("den")], writes=[B("scr2")])
            tt(dst, oan, sga_t[:], ALU.mult, [B("scr2"), sga_buf], [dstbuf])

        def swa_prompt(l, qT_t, qbuf, kprev, kprev_b, vprev, vprev_b, kcur, kcur_b, vcur, vcur_b):
            scb = ((0, 3), (4, 5))
            for kti, (kk, kb_) in enumerate(((kprev, kprev_b), (kcur, kcur_b))):
                for g in range(2):
                    bk = scb[kti][g]
                    mm(PB[bk][:, :], kk[g * 64:(g + 1) * 64, :], qT_t[g * 64:(g + 1) * 64, :, :], True, True, [kb_, qbuf], pbq(bk))
                    act(pTs[kti * 2 + g][:], PB[bk][:, :], AF.Exp, pbq(bk), [B("pTs%d" % (kti * 2 + g))], scale=0.125)
            for g in range(2):
                pp, pc = pTs[g], pTs[2 + g]
                rb = [B("pTs%d" % g), B("pTs%d" % (2 + g)), vprev_b, vcur_b]
                for j in range(4):
                    oA = PB[OAB[g]][0:64, j * 128:j * 128 + 65]
                    oB = PB[OAB[g]][64:128, j * 128:j * 128 + 65]
                    w = pbq(OAB[g], j * 128, j * 128 + 128)
                    mm(oA, pc[0:64, j * 128:j * 128 + 64], vcur[0:64, g, :], True, False, rb, w)
                    mm(oA, pp[:, j * 128:j * 128 + 64], vprev[:, g, :], False, True, rb, w)
                    mm(oB, pp[64:128, j * 128 + 64:j * 128 + 128], vprev[64:128, g, :], True, False, rb, w)
                    mm(oB, pc[:, j * 128 + 64:j * 128 + 128], vcur[:, g, :], False, True, rb, w)

        def swa_sample(l, kcur, kcur_b, vcur, vcur_b):
            gb = (0, 3)
            for b in range(4):
                for g in range(2):
                    off = b * 128
                    mm(PB[gb[g]][:, off:off + 128], ckT_bf[g * 64:(g + 1) * 64, b, :], qT[g * 64:(g + 1) * 64, :, 32 * b:32 * b + 32], True, True,
                       [B("ckT"), B("qT")], pbq(gb[g], off, off + 128))
            for g in range(2):
                act(pTs[g][:], PB[gb[g]][:, :], AF.Exp, pbq(gb[g]), [B("pTs%d" % g)], scale=0.125)
            nb = (4, 5)
            for b in range(4):
                for g in range(2):
                    mm(PB[nb[g]][32 * b:32 * b + 16, 0:128], kcur[g * 64:(g + 1) * 64, 32 * b:32 * b + 16],
                       qT[g * 64:(g + 1) * 64, :, 32 * b:32 * b + 32], True, True, [kcur_b, B("qT")], pbq(nb[g], 0, 128),
                       tp=(g * 64, 32 * b))
            for g in range(2):
                act(pTn[:, g * 128:(g + 1) * 128], PB[nb[g]][:, 0:128], AF.Exp, pbq(nb[g], 0, 128), [B("pTn")], scale=0.125)
            for b in range(4):
                for g in range(2):
                    pc = pTs[g]
                    rb = [B("pTs0"), B("pTs1"), B("pTn"), B("cvaug"), vcur_b]
                    for j in range(4):
                        o_ = PB[OAB[g]][32 * b:32 * b + 32, j * 128:j * 128 + 65]
                        w = pbq(OAB[g], j * 128, j * 128 + 128)
                        c0 = b * 128 + j * 32
                        mm(o_, pTn[32 * b:32 * b + 16, g * 128 + j * 32:g * 128 + j * 32 + 32], vcur[32 * b:32 * b + 16, g, :], True, False, rb, w,
                           tp=(32 * b, 32 * b))
                        mm(o_, pc[:, c0:c0 + 32], cvaug[:, b, g, :], False, True, rb, w, tp=(0, 32 * b))

        def phase_a1(l, t, gi, slot):
            sample = (t == NPT)
            so = spoA[t % 2]
            sob = B("spoA%d" % (t % 2))
            xnT = xng[gi % 2][:, :, slot * 128:(slot + 1) * 128]
            xnb = B("xng%d" % (gi % 2))
            tmg = [(0, 0, 0, 512), (1, 0, 512, 768), (2, 0, 768, 1280)]
            for kc in range(8):
                for (bk, off, c0, c1) in tmg:
                    mm(PB[bk][:, off:off + (c1 - c0)], xnT[:, kc, :], WX[:, kc, c0:c1], kc == 0, kc == 7,
                       [xnb, WXb[kc]], pbq(bk, off, off + (c1 - c0)))
            sqqk, qkf = scr1, scr2
            act(sqqk[:, 0:512], PB[0][:, :], AF.Square, pbq(0), [B("scr1")])
            act(sqqk[:, 512:640], PB[1][:, 0:128], AF.Square, pbq(1, 0, 128), [B("scr1")])
            red(ss[:, 0:10], sqqk.rearrange("p (h d) -> p h d", h=10), [B("scr1")], [B("ss")])
            act(rq[:, 0:10], ss[:, 0:10], AF.Ln, [B("ss")], [B("rq")], scale=1.0 / 64, bias=EPS)
            act(rq[:, 0:10], rq[:, 0:10], AF.Exp, [B("rq")], [B("rq")], scale=-0.5)
            tt(qkf[:, 0:512].rearrange("p (h d) -> p h d", h=8), PB[0][:, :].rearrange("p (h d) -> p h d", h=8),
               bcast(rq[:, 0:8], [(1, 8), (0, 64)]), ALU.mult, pbq(0) + [B("rq")], [B("scr2")])
            tt(qkf[:, 512:640].rearrange("p (h d) -> p h d", h=2), PB[1][:, 0:128].rearrange("p (h d) -> p h d", h=2),
               bcast(rq[:, 8:10], [(1, 2), (0, 64)]), ALU.mult, pbq(1, 0, 128) + [B("rq")], [B("scr2")])
            tt(qkf, qkf, qkg_sb[:], ALU.mult, [B("scr2"), B("qkg")], [B("scr2")])
            qv = qkf.rearrange("p (h d) -> p h d", h=10)
            x1, x2 = qv[:, :, 0:8], qv[:, :, 8:16]
            cosb = bcast(cos_sb[:, t * 8:(t + 1) * 8], [(0, 10), (1, 8)])
            sinb = bcast(sin_sb[:, t * 8:(t + 1) * 8], [(0, 10), (1, 8)])
            rr = [B("scr2"), B("cos"), B("sin")]
            tt(rt[:, 0], x1, cosb, ALU.mult, rr, [B("rt")])
            tt(rt[:, 1], x2, sinb, ALU.mult, rr, [B("rt")])
            tt(rt[:, 2], x2, cosb, ALU.mult, rr, [B("rt")])
            tt(rt[:, 3], x1, sinb, ALU.mult, rr, [B("rt")])
            tt(x1, rt[:, 0], rt[:, 1], ALU.subtract, [B("rt")], [B("scr2")])
            tt(x2, rt[:, 2], rt[:, 3], ALU.add, [B("rt")], [B("scr2")])
            cp(qtm[:, 0:4, :].rearrange("p j (half d) -> p j half d", half=2),
               qkf[:, 0:512].rearrange("p (half j d) -> p j half d", half=2, j=4), [B("scr2")], [B("qtm")])
            cp(qtm[:, 4, :], qkf[:, 512:640], [B("scr2")], [B("qtm")])
            for j in range(5):
                tr(PT7[:, j, :], qtm[:, j, :], [B("qtm")], [B("pt7")])
            if t == 0:
                qT_t, qbuf, kc_t, kc_b, vc_t, vc_b = qT0, B("qT0"), kT0, B("kT0"), vaug0, B("vaug0")
            else:
                qT_t, qbuf, kc_t, kc_b, vc_t, vc_b = qT, B("qT"), kTr[t % 2], B("kTr%d" % (t % 2)), vaugr[t % 2], B("vaugr%d" % (t % 2))
            cp(qT_t[:], PT7[:, 0:4, :], [B("pt7")], [qbuf], eng="act")
            cp(kc_t[:], PT7[:, 4, :], [B("pt7")], [kc_b], eng="act")
            cp(vc_t[:, :, 0:64], PB[1][:, 128:256].rearrange("p (g d) -> p g d", g=2), pbq(1, 128, 256), [vc_b])
            if t >= NPT - 1:
                which = 0 if t == NPT - 1 else 1
                dma("sp", nk_o[l, which], qkf[:, 512:640], [B("scr2")], [B("nk_o")], "nk")
                cp(vf32[:], PB[1][:, 128:256], pbq(1, 128, 256), [B("vf32")], eng="act")
                dma("sp", nv_o[l, which], vf32[:], [B("vf32")], [B("nv_o")], "nv")
            sga_t, sga_b = (sga0, B("sga0")) if t == 0 else (sga, B("sga"))
            sigmoid_act(eg, B("eg"), PB[2][:, :], pbq(2))
            tt(sga_t[:], PB[2][:, :], eg, ALU.mult, pbq(2) + [B("eg")], [sga_b])
            if sample:
                swa_sample(l, kc_t, kc_b, vc_t, vc_b)
                swa_finish(l, so, sob, sga_t, sga_b)
            elif t >= 1:
                if t == 1:
                    kp, kpb, vp, vpb = kT0, B("kT0"), vaug0, B("vaug0")
                else:
                    kp, kpb, vp, vpb = kTr[(t - 1) % 2], B("kTr%d" % ((t - 1) % 2)), vaugr[(t - 1) % 2], B("vaugr%d" % ((t - 1) % 2))
                swa_prompt(l, qT_t, qbuf, kp, kpb, vp, vpb, kc_t, kc_b, vc_t, vc_b)
                swa_finish(l, so, sob, sga_t, sga_b)
            else:
                mset(so, 0.0, [sob])
            if t == NPT - 1:
                cp(pay[:, 0:128], kc_t[:], [kc_b], [B("pay")])
                cp(pay[:, 128:258], vc_t[:].rearrange("p g d -> p (g d)"), [vc_b], [B("pay")])
            dma("sp", spill_d[t][:, 0:512], so, [sob], [B("spillA%d" % t)], "spoA%d" % (t % 2))

        def phase_a2(l, t, gi, slot):
            sample = (t == NPT)
            so = spoB[t % 2]
            sob = B("spoB%d" % (t % 2))
            xnT = xng[gi % 2][:, :, slot * 128:(slot + 1) * 128]
            xnb = B("xng%d" % (gi % 2))
            tmg = [(0, 0, 256, 512), (3, 0, 512, 1024), (4, 0, 1024, 1536)]
            for kc in range(8):
                for (bk, off, c0, c1) in tmg:
                    mm(PB[bk][:, off:off + (c1 - c0)], xnT[:, kc, :], WY[:, kc, c0:c1], kc == 0, kc == 7,
                       [xnb, WYb[kc]], pbq(bk, off, off + (c1 - c0)))
            for j in range(4):
                for kc in range(8):
                    mm(PB[5][:, j * 128:(j + 1) * 128], WY[:, kc, j * 128:(j + 1) * 128], xnT[:, kc, :], kc == 0, kc == 7,
                       [xnb, WYb[kc]], pbq(5, j * 128, j * 128 + 128))
            for kc in range(8):
                mm(PB[6][0:16, 128:256], WY[:, kc, 1536:1552], xnT[:, kc, :], kc == 0, kc == 7, [xnb, WYb[kc]], pbq(6, 128, 256))
            sigmoid_act(eg2, B("eg2"), PB[4][:, :], pbq(4))
            tt(eg2, PB[4][:, :], eg2, ALU.mult, pbq(4) + [B("eg2")], [B("eg2")])
            tt(so[:, 0:512], eg2, gng_sb[:], ALU.mult, [B("eg2"), B("gng")], [sob])
            cp(abT_aug[0:16, :], PB[6][0:16, 128:256], pbq(6, 128, 256), [B("abT_aug")], eng="act")
            mm(PB[6][:, 256:512], abT_aug[0:32, :], wg_sb[0:32, l * 256:(l + 1) * 256], True, True, [B("abT_aug"), B("wg")], pbq(6, 256, 512))
            act(lg[:], PB[6][:, 256:512], AF.Exp, pbq(6, 256, 512), [B("lg")], scale=-1.0)
            act(lg[:], lg[:], AF.Ln, [B("lg")], [B("lg")], bias=1.0)
            cp(lhi[:], lg[:], [B("lg")], [B("lhi")])
            tt(llo[:], lg[:], lhi[:], ALU.subtract, [B("lg"), B("lhi")], [B("llo")])
            U = U_s if sample else U_p
            mG = mG_s if sample else mG_p
            mm(PB[2][:, 0:256], U, lhi[:], True, False, [B("cst"), B("lhi")], pbq(2, 0, 256))
            mm(PB[2][:, 0:256], U, llo[:], False, True, [B("cst"), B("llo")], pbq(2, 0, 256))
            for c in range(2):
                mm(PB[2][:, 256 + c * 128:384 + c * 128], lhi[:, c * 128:(c + 1) * 128], U, True, False, [B("cst"), B("lhi")], pbq(2, 256 + c * 128, 384 + c * 128))
                mm(PB[2][:, 256 + c * 128:384 + c * 128], llo[:, c * 128:(c + 1) * 128], U, False, True, [B("cst"), B("llo")], pbq(2, 256 + c * 128, 384 + c * 128))
            act(ebT[:], PB[2][:, 256:512], AF.Exp, pbq(2, 256, 512), [B("ebT")])
            act(enbT[:], PB[2][:, 256:512], AF.Exp, pbq(2, 256, 512), [B("enbT")], scale=-1.0)
            act(enb[:], PB[2][:, 0:256], AF.Exp, pbq(2, 0, 256), [B("enb")], scale=-1.0)
            bTv = PB[2][:, 256:512].rearrange("p (c t) -> p c t", c=2)
            if not sample:
                act(Dall[:, t, :], bTv[:, :, 127], AF.Exp, pbq(2, 256, 512), [B("Dall")])
            else:
                for b in range(4):
                    act(Ds[:, b, :], bTv[:, :, 32 * b + 15], AF.Exp, pbq(2, 256, 512), [B("Ds")])
            qts = so[:, 1024:1280]
            stt(qts, PB[5][:, 0:256], 0.125, ebT[:], ALU.mult, ALU.mult, pbq(5, 0, 256) + [B("ebT")], [sob])
            tt(ktT[:], PB[5][:, 256:512], enbT[:], ALU.mult, pbq(5, 256, 512) + [B("enbT")], [B("ktT")])
            tt(kt[:], PB[0][:, 0:256], enb[:], ALU.mult, pbq(0, 0, 256) + [B("enb")], [B("kt")])
            cp(vbb[:], PB[3][:, :], pbq(3), [B("vbb")], eng="act")
            ab_ = (5, 0)
            for h in range(4):
                hp, hc = h % 2, h // 2
                mm(PB[ab_[hp]][:, hc * 128:(hc + 1) * 128], ktT[hp * 64:(hp + 1) * 64, hc * 128:(hc + 1) * 128],
                   qts[hp * 64:(hp + 1) * 64, hc * 128:(hc + 1) * 128], True, True, [B("ktT"), sob], pbq(ab_[hp], hc * 128, hc * 128 + 128))
            attv = attb[:].rearrange("p (hc hp) t -> p hc hp t", hp=2)
            for hp in range(2):
                tt(attv[:, :, hp, :], PB[ab_[hp]][:, 0:256].rearrange("p (c t) -> p c t", c=2), bcast(mG, [(0, 2), (1, 128)]), ALU.mult,
                   pbq(ab_[hp], 0, 256) + [B("cst")], [B("attb")])
            if sample:
                for b in range(4):
                    tt(qmask[:, b, :].rearrange("p (c t) -> p c t", c=2), qts.rearrange("p (c t) -> p c t", c=2),
                       bcast(cst[:, (6 + b) * 128:(7 + b) * 128], [(0, 2), (1, 128)]), ALU.mult, [sob, B("cst")], [B("qmask")])
                    ts(vm[:, b, :], vbb[:], selmsk[:, 16 + b:17 + b], ALU.mult, [B("vbb"), B("selmsk")], [B("pTs%d" % b)])
            for h in range(4):
                hp, hc = h % 2, h // 2
                w = pbq(3, h * 128, h * 128 + 128)
                mm(PB[3][:, h * 128:(h + 1) * 128], attb[:, h, :], vbb[:, h * 128:(h + 1) * 128], True, False, [B("attb"), B("vbb")], w)
                if not sample:
                    mm(PB[3][:, h * 128:(h + 1) * 128], qts[hp * 64:(hp + 1) * 64, hc * 128:(hc + 1) * 128], Sbf[hp * 64:(hp + 1) * 64, hc, :],
                       False, True, [sob, B("Sbf")], w)
                else:
                    for b in range(4):
                        mm(PB[3][:, h * 128:(h + 1) * 128], qmask[hp * 64:(hp + 1) * 64, b, hc * 128:(hc + 1) * 128],
                           S0bf[hp * 64:(hp + 1) * 64, b, hc, :], False, b == 3, [B("qmask"), B("S0bf")], w)
            cp(so[:, 512:1024], PB[3][:, :], pbq(3), [sob], eng="act")
            if not sample:
                for h in range(4):
                    hp, hc = h % 2, h // 2
                    mm(PB[1][hp * 64:(hp + 1) * 64, hc * 128:(hc + 1) * 128], kt[:, h * 64:(h + 1) * 64], vbb[:, h * 128:(h + 1) * 128], True, True,
                       [B("kt"), B("vbb")], pbq(1, hc * 128, hc * 128 + 128))
                tt(S[:], PB[1][:, 0:256], S[:], ALU.add, pbq(1, 0, 256) + [B("S")], [B("S")])
                for c in range(2):
                    ts(S[:, c * 128:(c + 1) * 128], S[:, c * 128:(c + 1) * 128], Dall[:, t, c:c + 1], ALU.mult, [B("S"), B("Dall")], [B("S")])
                cp(Sbf[:], S[:].rearrange("p (c v) -> p c v", c=2), [B("S")], [B("Sbf")])
                tt(Dtot[:], Dtot[:], Dall[:, t, :], ALU.mult, [B("Dtot"), B("Dall")], [B("Dtot")])
            else:
                for b in range(4):
                    bk = 1 if b % 2 == 0 else 0
                    for h in range(4):
                        hp, hc = h % 2, h // 2
                        off = (b // 2) * 256 + hc * 128
                        mm(PB[bk][hp * 64:(hp + 1) * 64, off:off + 128], kt[:, h * 64:(h + 1) * 64],
                           vm[:, b, h * 128:(h + 1) * 128], True, True, [B("kt"), B("pTs%d" % b)], pbq(bk, off, off + 128))
                for b in range(4):
                    bk = 1 if b % 2 == 0 else 0
                    off = (b // 2) * 256
                    s0b = S0f[:, b, :, :].rearrange("p c v -> p (c v)")
                    tt(s0b, PB[bk][:, off:off + 256], s0b, ALU.add, pbq(bk, off, off + 256) + [B("S0f")], [B("S0f")])
                    for c in range(2):
                        ts(S0f[:, b, c, :], S0f[:, b, c, :], Ds[:, b, c:c + 1], ALU.mult, [B("S0f"), B("Ds")], [B("S0f")])
                dma("sp", ss_o[l], S0f[:].rearrange("p b c v -> p (b c v)"), [B("S0f")], [B("ss_o")], "ss_o")
            dma("sp", spill_d[t][:, 512:1792], so, [sob], [B("spillB%d" % t)], "spoB%d" % (t % 2))

        def exchange(l):
            cp(pay[:, 258:514], S[:], [B("S")], [B("pay")])
            cp(pay[:, 514:516], Dtot[:], [B("Dtot")], [B("pay")])
            dma("sp", ag_in_d[l], pay[:], [B("pay")], [B("ag_in%d" % l)], "agi")
            P.op("pool", lambda e: e.collective_compute("AllGather", ALU.bypass, replica_groups=[list(range(NCORES))],
                                                       ins=[ag_in_d[l]], outs=[ag_out_d[l]]),
                 reads=[B("ag_in%d" % l)], writes=[B("ag_out%d" % l)], cc=True, semkey="cc")
            sel, msk = selmsk[:, 0:8], selmsk[:, 8:16]
            mset(Sc[:], 0.0, [B("Sc")])
            agv = ag_out_d[l].rearrange("(r p) n -> p r n", p=128)
            for hf in range(2):
                dma("sp", gath, agv[:, hf * 4:(hf + 1) * 4, :], [B("ag_out%d" % l)], SCRB, "ago")
                for jj in range(4):
                    j = hf * 4 + jj
                    if j == 0:
                        ts(hk[:], gath[:, 0, 0:258], sel[:, 0:1], ALU.mult, SCRB + [B("selmsk")], [B("hk")])
                    else:
                        stt(hk[:], gath[:, jj, 0:258], sel[:, j:j + 1], hk[:], ALU.mult, ALU.add, SCRB + [B("selmsk"), B("hk")], [B("hk")])
                fh = fac[:, hf * 4:(hf + 1) * 4, :]
                mh = msk[:, hf * 4:(hf + 1) * 4]
                ts(fh, gath[:, :, 514:516], -1.0, ALU.add, SCRB, [B("fac")])
                tt(fh, fh, bcast(mh, [(1, 4), (0, 2)]), ALU.mult, [B("fac"), B("selmsk")], [B("fac")])
                ts(fh, fh, 1.0, ALU.add, [B("fac")], [B("fac")])
                tt(gath[:, :, 258:514], gath[:, :, 258:514], bcast(mh, [(1, 4), (0, 256)]), ALU.mult, SCRB + [B("selmsk")], SCRB)
                for jj in range(4):
                    j = hf * 4 + jj
                    for c in range(2):
                        stt(Sc[:, c * 128:(c + 1) * 128], Sc[:, c * 128:(c + 1) * 128], fac[:, j, c:c + 1], gath[:, jj, 258 + c * 128:386 + c * 128],
                            ALU.mult, ALU.add, [B("Sc"), B("fac")] + SCRB, [B("Sc")])
            cp(kTh[:], hk[:, 0:128], [B("hk")], [B("kTh")])
            cp(vaugh[:].rearrange("p g d -> p (g d)"), hk[:, 128:258], [B("hk")], [B("vaugh")])
            cp(Scbf[:], Sc[:].rearrange("p (c v) -> p c v", c=2), [B("Sc")], [B("Scbf")])
            for c in range(2):
                stt(Send[:, c * 128:(c + 1) * 128], Sc[:, c * 128:(c + 1) * 128], Dtot[:, c:c + 1], S[:, c * 128:(c + 1) * 128], ALU.mult, ALU.add,
                    [B("Sc"), B("Dtot"), B("S")], [B("Send")])
            dma("sp", sp_o[l], Send[:], [B("Send")], [B("sp_o")], "sp_o")

        def prefetch_b1(l, t):
            k = t % 2
            dma("sp", spi[k][:], spill_d[t], [B("spillA%d" % t), B("spillB%d" % t)], [B("spoA%d" % k), B("spoB%d" % k)], "spi%d" % k)

        def b1_pre(l, t, gi, slot):
            sample = (t == NPT)
            si = spi[t % 2]
            sibs = [B("spoA%d" % (t % 2)), B("spoB%d" % (t % 2))]
            if t + 1 < NT:
                prefetch_b1(l, t + 1)
            if t == 0:
                swa_prompt(l, qT0, B("qT0"), kTh, B("kTh"), vaugh, B("vaugh"), kT0, B("kT0"), vaug0, B("vaug0"))
                swa_finish(l, si[:, 0:512], sibs[0], sga0, B("sga0"))
            of, osq = scr2[:, 0:512], scr1[:, 0:512]
            qts = si[:, 1536:1792]
            if not sample:
                cb_ = (3, 5)
                for h in range(4):
                    hp, hc = h % 2, h // 2
                    mm(PB[cb_[hp]][:, hc * 128:(hc + 1) * 128], qts[hp * 64:(hp + 1) * 64, hc * 128:(hc + 1) * 128], Scbf[hp * 64:(hp + 1) * 64, hc, :],
                       True, True, sibs + [B("Scbf")], pbq(cb_[hp]))
                ofv = of.rearrange("p (hc hp v) -> p hc hp v", hc=2, hp=2)
                olv = si[:, 1024:1536].rearrange("p (hc hp v) -> p hc hp v", hc=2, hp=2)
                for hp in range(2):
                    tt(ofv[:, :, hp, :], PB[cb_[hp]][:, 0:256].rearrange("p (c v) -> p c v", c=2), olv[:, :, hp, :], ALU.add,
                       pbq(cb_[hp]) + sibs, [B("scr2")])
                for c in range(2):
                    ts(Sc[:, c * 128:(c + 1) * 128], Sc[:, c * 128:(c + 1) * 128], Dall[:, t, c:c + 1], ALU.mult, [B("Sc"), B("Dall")], [B("Sc")])
                cp(Scbf[:], Sc[:].rearrange("p (c v) -> p c v", c=2), [B("Sc")], [B("Scbf")])
            else:
                cp(of, si[:, 1024:1536], sibs, [B("scr2")])
            act(osq, of, AF.Square, [B("scr2")], [B("scr1")])
            red(oss[:, 0:4], osq.rearrange("p (h d) -> p h d", h=4), [B("scr1")], [B("oss")])
            act(ro[:, 0:4], oss[:, 0:4], AF.Ln, [B("oss")], [B("ro")], scale=1.0 / 128, bias=EPS)
            act(ro[:, 0:4], ro[:, 0:4], AF.Exp, [B("ro")], [B("ro")], scale=-0.5)
            tt(of.rearrange("p (h d) -> p h d", h=4), of.rearrange("p (h d) -> p h d", h=4), bcast(ro[:, 0:4], [(1, 4), (0, 128)]), ALU.mult,
               [B("scr2"), B("ro")], [B("scr2")])
            tt(mixb[:], of, si[:, 512:1024], ALU.mult, [B("scr2")] + sibs, [B("mixb")])
            for j in range(4):
                tr(PT7[:, j, :], si[:, j * 128:(j + 1) * 128], sibs, [B("pt7")])
            for j in range(4):
                tr(PT7[:, 4 + j, :], mixb[:, j * 128:(j + 1) * 128], [B("mixb")], [B("pt7")])
            k = gi % 2
            cp(sqg[k][:, :, slot * 128:(slot + 1) * 128], PT7[:], [B("pt7")], [B("sqg%d" % k)], eng="act")

        def b1_body(l, gi, ocs):
            tiles = GROUPS[gi]
            k = gi % 2
            c0, N = tiles[0] * 128, 128 * len(tiles)
            hbs = [hb[t] for t in tiles]
            for oc in ocs:
                bk = oc % 2
                for kc in range(8):
                    mm(PB[bk][:, 0:N], WX[:, kc, oc * 128:(oc + 1) * 128], sqg[k][:, kc, 0:N], kc == 0, kc == 7,
                       [WXb[kc], B("sqg%d" % k)], pbq(bk))
                tt(hT[:, oc, c0:c0 + N], hT[:, oc, c0:c0 + N], PB[bk][:, 0:N], ALU.add, hbs + pbq(bk), hbs)

        def b2_front(l, gi):
            tiles = GROUPS[gi]
            k = gi % 2
            c0, N = tiles[0] * 128, 128 * len(tiles)
            norm_group(gi, peg_sb, l)
            dma("sp", pTf[:, :, 0:N], pT_d[l, :, c0:c0 + N].rearrange("(kc p) t -> p kc t", p=128), [], [B("pTf")], "pTf")
            cp(pTb[k][:, :, 0:N], pTf[:, :, 0:N], [B("pTf")], [B("pTb%d" % k)], eng="act")

        def b2_body(l, gi, ocs, last):
            tiles = GROUPS[gi]
            k = gi % 2
            c0, N = tiles[0] * 128, 128 * len(tiles)
            hbs = [hb[t] for t in tiles]
            for oc in ocs:
                G, E = (2, 4)[oc % 2], (5, 3)[oc % 2]
                e_t, e_b = ((eg, B("eg")) if oc % 2 == 0 else (eg2, B("eg2")))
                e_t = e_t[:, 0:N]
                for kc in range(8):
                    mm(PB[G][:, 0:N], WY[:, kc, oc * 128:(oc + 1) * 128], xng[k][:, kc, 0:N], kc == 0, kc == 7,
                       [WYb[kc], B("xng%d" % k)], pbq(G))
                for kc in range(2):
                    q = kc * 2 + oc // 4
                    mm(PB[E][:, 0:N], WY[:, q, 1024 + (oc % 4) * 128:1024 + (oc % 4) * 128 + 128], pTb[k][:, kc, 0:N], kc == 0, kc == 1,
                       [WYb[q], B("pTb%d" % k)], pbq(E))
                sigmoid_act(e_t, e_b, PB[G][:, 0:N], pbq(G))
                tt(e_t, PB[E][:, 0:N], e_t, ALU.mult, pbq(E) + [e_b], [e_b])
                tt(hT[:, oc, c0:c0 + N], hT[:, oc, c0:c0 + N], e_t, ALU.add, hbs + [e_b], hbs)
            if last and ocs[-1] == 7:
                for t in tiles:
                    t0, t1 = t * 128, (t + 1) * 128
                    dma("sp", yT_o[:, t0:t1].rearrange("(kc p) t -> p kc t", p=128), hT[:, :, t0:t1], [hb[t]], [B("y_o")], "y")

        NG = len(GROUPS)
        load_wx_in(0)
        for l in range(depth):
            last = (l == depth - 1)
            load_layer_small(l)
            load_wy_in(l)
            mset(S[:], 0.0, [B("S")])
            mset(Sbf[:], 0.0, [B("Sbf")])
            mset(Dtot[:], 1.0, [B("Dtot")])
            for gi, tiles in enumerate(GROUPS):
                norm_group(gi, ng_sb, l)
                for slot, t in enumerate(tiles):
                    phase_a1(l, t, gi, slot)
            load_wx_out(l)
            for gi, tiles in enumerate(GROUPS):
                norm_group(gi, ng_sb, l)
                for slot, t in enumerate(tiles):
                    phase_a2(l, t, gi, slot)
            exchange(l)
            load_wy_pg(l)
            prefetch_b1(l, 0)
            for slot, t in enumerate(GROUPS[0]):
                b1_pre(l, t, 0, slot)
            for gi in range(NG):
                b1_body(l, gi, range(0, 4))
                if gi + 1 < NG:
                    b1_pre(l, GROUPS[gi + 1][0], gi + 1, 0)
                b1_body(l, gi, range(4, 8))
                if gi + 1 < NG and len(GROUPS[gi + 1]) > 1:
                    b1_pre(l, GROUPS[gi + 1][1], gi + 1, 1)
            if l + 1 < depth:
                load_wx_in(l + 1)
            b2_front(l, 0)
            for gi in range(NG):
                b2_body(l, gi, range(0, 4), last)
                if gi + 1 < NG:
                    b2_front(l, gi + 1)
                b2_body(l, gi, range(4, 8), last)
        print('NOPS', len(P.ops))
        P.emit(nc)
    return nc


def _consts():
    s = np.arange(128)[:, None]
    t = np.arange(128)[None, :]
    tri = (s <= t)
    blk = (s // 32) == (t // 32)
    ident = np.eye(128, dtype=np.float32)
    ones = np.ones((128, 128), np.float32)
    U_p = np.where(tri, -1.0 / 16, 0.0).astype(np.float32)
    U_s = np.where(tri & blk, -1.0 / 16, 0.0).astype(np.float32)
    mG_p = tri.astype(np.float32)
    mG_s = (tri & blk).astype(np.float32)
    slot = [np.broadcast_to(((np.arange(128) // 32) == b).astype(np.float32)[None, :], (128, 128)) for b in range(4)]
    return np.ascontiguousarray(np.concatenate([ident, ones, U_p, U_s, mG_p, mG_s] + slot, axis=1))


def _rope_tables(core):
    half = 8
    inv = np.power(np.float32(500000.0), -np.arange(half, dtype=np.float32) * np.float32(2.0 / 16)).astype(np.float32)
    pos = np.zeros((128, NT), np.float32)
    for t in range(NPT):
        pos[:, t] = core * TPC + t * 128 + np.arange(128)
    for b in range(4):
        pos[32 * b:32 * b + 16, NPT] = PAST + np.arange(16)
    ang = (pos[:, :, None] * inv[None, None, :]).astype(np.float32)
    return np.cos(ang).astype(np.float32).reshape(128, NT * 8), np.sin(ang).astype(np.float32).reshape(128, NT * 8)


_NC_CACHE = {}


def kernel(x_prompt, x_sample, cache_k, cache_v, state_gla, p_prompt, p_sample, norm_g, w_in, q_norm_g, k_norm_g, sinks,
           w_gate_up, b_gate, gla_norm_g, w_out, pe_norm_g, w_pe, w_pg, _depth=DEPTH):
    depth = _depth
    f = lambda a: np.ascontiguousarray(np.asarray(a, dtype=np.float32))
    x_prompt, x_sample, cache_k, cache_v, state_gla, p_prompt, p_sample = map(f, (x_prompt, x_sample, cache_k, cache_v, state_gla, p_prompt, p_sample))
    norm_g, w_in, q_norm_g, k_norm_g, sinks, w_gate_up, b_gate, gla_norm_g, w_out, pe_norm_g, w_pe, w_pg = map(
        f, (norm_g, w_in, q_norm_g, k_norm_g, sinks, w_gate_up, b_gate, gla_norm_g, w_out, pe_norm_g, w_pe, w_pg))
    if depth not in _NC_CACHE:
        _NC_CACHE[depth] = build(depth)
    nc = _NC_CACHE[depth]

    cst = _consts()
    rep = lambda a: np.ascontiguousarray(np.broadcast_to(a.reshape(1, -1), (128, a.size)))
    ng = np.ascontiguousarray(norm_g[:depth].reshape(depth, 8, 128).transpose(2, 0, 1).reshape(128, depth * 8))
    peg = np.ascontiguousarray(pe_norm_g[:depth].reshape(depth, 8, 128).transpose(2, 0, 1).reshape(128, depth * 8))
    qkg = rep(np.concatenate([np.tile(q_norm_g[:depth], (1, 8)), np.tile(k_norm_g[:depth], (1, 2))], axis=1))
    gng = rep(np.tile(gla_norm_g[:depth], (1, 4)))
    snk = rep(sinks[:depth])
    wg = np.zeros((32, depth, 256), np.float32)
    wg[0:16] = w_gate_up[:depth].transpose(1, 0, 2)
    wg[16] = b_gate[:depth]
    wg = wg.reshape(32, depth * 256)
    w_in_c, w_out_c, w_pg_c, w_pe_c = w_in[:depth], w_out[:depth], w_pg[:depth], w_pe[:depth]

    in_maps = []
    for c in range(NCORES):
        xT = np.zeros((DM, TOK), np.float32)
        xT[:, 0:TPC] = x_prompt[0, c * TPC:(c + 1) * TPC].T
        pT = np.zeros((depth, 256, TOK), np.float32)
        pT[:, :, 0:TPC] = p_prompt[:depth, 0, c * TPC:(c + 1) * TPC].transpose(0, 2, 1)
        for b in range(4):
            sq_ = 4 * c + b
            xT[:, TPC + 32 * b:TPC + 32 * b + 16] = x_sample[sq_].T
            pT[:, :, TPC + 32 * b:TPC + 32 * b + 16] = p_sample[:depth, sq_].transpose(0, 2, 1)
        ckT = np.ascontiguousarray(cache_k[:depth, 4 * c:4 * c + 4].reshape(depth, 4, 128, 128).transpose(0, 1, 3, 2))
        cv = np.ascontiguousarray(cache_v[:depth, 4 * c:4 * c + 4].reshape(depth, 4, 128, 128))
        s0 = np.ascontiguousarray(state_gla[:depth, 4 * c:4 * c + 4])
        cosT, sinT = _rope_tables(c)
        selmsk = np.zeros((128, 20), np.float32)
        for b in range(4):
            selmsk[32 * b:32 * b + 32, 16 + b] = 1.0
        if c >= 1:
            selmsk[:, c - 1] = 1.0
        selmsk[:, 8:8 + c] = 1.0
        in_maps.append({
            "xT": xT, "pT": pT, "ckT": ckT, "cv": cv, "s0": s0, "cosT": cosT, "sinT": sinT, "cst": cst, "selmsk": selmsk,
            "ng": ng, "peg": peg, "qkg": qkg, "gng": gng, "snk": snk, "wg": wg,
            "w_in": w_in_c, "w_out": w_out_c, "w_pg": w_pg_c, "w_pe": w_pe_c,
        })
    res = run_bass_kernel_spmd(nc, in_maps, core_ids=list(range(NCORES)))
    R = res.results

    y_prompt = np.zeros((1, SEQ, DM), np.float32)
    y_sample = np.zeros((32, 16, DM), np.float32)
    nks = np.zeros((depth, 32, 16, 2, 64), np.float32)
    nvs = np.zeros((depth, 32, 16, 2, 64), np.float32)
    nss = np.zeros((depth, 32, 4, 64, 128), np.float32)
    for c in range(NCORES):
        yT = R[c]["yT"]
        y_prompt[0, c * TPC:(c + 1) * TPC] = yT[:, 0:TPC].T
        for b in range(4):
            y_sample[4 * c + b] = yT[:, TPC + 32 * b:TPC + 32 * b + 16].T
            nks[:, 4 * c + b] = R[c]["nk"][:, 1, 32 * b:32 * b + 16].reshape(depth, 16, 2, 64)
            nvs[:, 4 * c + b] = R[c]["nv"][:, 1, 32 * b:32 * b + 16].reshape(depth, 16, 2, 64)
        st = R[c]["st_s"].reshape(depth, 2, 64, 4, 2, 128)
        nss[:, 4 * c:4 * c + 4] = st.transpose(0, 3, 4, 1, 2, 5).reshape(depth, 4, 4, 64, 128)
    last = R[NCORES - 1]
    nkp = last["nk"][:, 0].reshape(depth, 1, 128, 2, 64).copy()
    nvp = last["nv"][:, 0].reshape(depth, 1, 128, 2, 64).copy()
    stp = last["st_p"].reshape(depth, 2, 64, 2, 128).transpose(0, 3, 1, 2, 4).reshape(depth, 1, 4, 64, 128).copy()
    return (y_prompt, y_sample, nkp, nvp, stp, nks, nvs, nss)
```
